# Optimizing a Trainium2 kernel written in Bass

```python
import jax, jax.numpy as jnp
from jax import lax
import numpy as np

D_MODEL = 1024
BATCH = 2
SEQ = 16384
DEPTH = 4

FOX_HEADS = 8
FOX_HEAD_DIM = 64
FOX_WIDTH = FOX_HEADS * FOX_HEAD_DIM
RET_HEADS = 4
RET_HEAD_DIM = 128
RET_WIDTH = RET_HEADS * RET_HEAD_DIM
MIX_WIDTH = FOX_WIDTH + RET_WIDTH
IN_COLS = 3 * FOX_WIDTH + FOX_HEADS + 4 * RET_WIDTH
D_FF = 4 * D_MODEL
Q_BLOCK = 128
RET_CHUNK = 128
ROPE_BASE = 10000.0
LN_EPS = 1e-5
GN_EPS = 1e-5
DEEPNORM_ALPHA = (2 * DEPTH) ** 0.25
DEEPNORM_BETA = (8 * DEPTH) ** -0.25

kernel_name = "fox_retention_hybrid_deepnorm"

SPLITS = [FOX_WIDTH, 2 * FOX_WIDTH, 3 * FOX_WIDTH, 3 * FOX_WIDTH + FOX_HEADS,
          3 * FOX_WIDTH + FOX_HEADS + RET_WIDTH, 3 * FOX_WIDTH + FOX_HEADS + 2 * RET_WIDTH,
          3 * FOX_WIDTH + FOX_HEADS + 3 * RET_WIDTH]


def layernorm(x, g, b):
    xf = x.astype(jnp.float32)
    mu = jnp.mean(xf, axis=-1, keepdims=True)
    var = jnp.mean(jnp.square(xf - mu), axis=-1, keepdims=True)
    y = (xf - mu) * lax.rsqrt(var + LN_EPS)
    return (y * g + b).astype(x.dtype)


def rotary(x):
    s, d = x.shape[1], x.shape[-1]
    half = d // 2
    inv_freq = ROPE_BASE ** (-jnp.arange(half, dtype=jnp.float32) / half)
    ang = jnp.arange(s, dtype=jnp.float32)[:, None] * inv_freq[None, :]
    cos = jnp.cos(ang)[None, :, None, :]
    sin = jnp.sin(ang)[None, :, None, :]
    x1, x2 = x[..., :half].astype(jnp.float32), x[..., half:].astype(jnp.float32)
    return jnp.concatenate([x1 * cos - x2 * sin, x1 * sin + x2 * cos], axis=-1).astype(x.dtype)


def fox_attention(q, k, v, log_f):
    b, s, h, d = q.shape
    n_blocks = s // Q_BLOCK
    scale = d ** -0.5
    c = jnp.cumsum(log_f.astype(jnp.float32), axis=1).transpose(0, 2, 1)
    qb = q.reshape(b, n_blocks, Q_BLOCK, h, d).transpose(1, 0, 2, 3, 4)
    cb = c.reshape(b, h, n_blocks, Q_BLOCK).transpose(2, 0, 1, 3)
    key_pos = jnp.arange(s)

    def block(args):
        i, q_i, c_i = args
        scores = jnp.einsum('bqhd,bkhd->bhqk', q_i, k).astype(jnp.float32) * scale
        bias = c_i[..., :, None] - c[..., None, :]
        q_pos = i * Q_BLOCK + jnp.arange(Q_BLOCK)
        causal = key_pos[None, :] <= q_pos[:, None]
        logits = jnp.where(causal, scores + bias, -jnp.inf)
        p = jax.nn.softmax(logits, axis=-1)
        return jnp.einsum('bhqk,bkhd->bqhd', p.astype(v.dtype), v)

    out = lax.map(block, (jnp.arange(n_blocks), qb, cb))
    return out.transpose(1, 0, 2, 3, 4).reshape(b, s, h * d)


def retention_chunkwise(q, k, v, gamma):
    b, s, h, dk = q.shape
    dv = v.shape[-1]
    n_chunks = s // RET_CHUNK
    log_g = jnp.log(gamma)
    idx = jnp.arange(RET_CHUNK, dtype=jnp.float32)
    diff = idx[:, None] - idx[None, :]
    decay_mask = jnp.where(diff >= 0,
                           jnp.exp(log_g[:, None, None] * jnp.maximum(diff, 0.0)), 0.0)
    xi = jnp.exp(log_g[:, None] * (idx + 1.0))[..., None]
    zeta = jnp.exp(log_g[:, None] * (RET_CHUNK - 1.0 - idx))[..., None]
    chunk_decay = jnp.exp(log_g * RET_CHUNK)[:, None, None]

    def to_chunks(t):
        return t.reshape(b, n_chunks, RET_CHUNK, h, t.shape[-1]).transpose(1, 0, 3, 2, 4)

    def step(state, inp):
        q_i, k_i, v_i = inp
        inner = jnp.einsum('bhqd,bhkd->bhqk', q_i, k_i) * decay_mask
        o = (jnp.einsum('bhqk,bhkv->bhqv', inner, v_i)
             + jnp.einsum('bhqd,bhdv->bhqv', q_i * xi, state))
        state = chunk_decay * state + jnp.einsum('bhkd,bhkv->bhdv', k_i * zeta, v_i)
        return state, o

    state0 = jnp.zeros((b, h, dk, dv), jnp.float32)
    _, out = lax.scan(step, state0, (to_chunks(q), to_chunks(k), to_chunks(v)))
    return out.transpose(1, 0, 3, 2, 4).reshape(b, s, h, dv)


def head_groupnorm(x, g):
    xf = x.astype(jnp.float32)
    mu = jnp.mean(xf, axis=-1, keepdims=True)
    var = jnp.mean(jnp.square(xf - mu), axis=-1, keepdims=True)
    return ((xf - mu) * lax.rsqrt(var + GN_EPS) * g).astype(x.dtype)


def hybrid_layer(x, w_in, w_out, w_ff1, w_ff2, ln1_g, ln1_b, ln2_g, ln2_b, b_forget, ret_gn_g, gamma):
    b, s, _ = x.shape
    proj = jnp.einsum('bsd,dc->bsc', x, w_in)
    fq, fk, fv, f_logit, rq, rk, rv, rg = jnp.split(proj, SPLITS, axis=-1)

    fshape = (b, s, FOX_HEADS, FOX_HEAD_DIM)
    log_f = jax.nn.log_sigmoid((f_logit + b_forget).astype(jnp.float32))
    fox_out = fox_attention(fq.reshape(fshape), fk.reshape(fshape), fv.reshape(fshape), log_f)

    rshape = (b, s, RET_HEADS, RET_HEAD_DIM)
    q_r = rotary(rq.reshape(rshape))
    k_r = rotary(rk.reshape(rshape)) * (RET_HEAD_DIM ** -0.5)
    ret = retention_chunkwise(q_r, k_r, rv.reshape(rshape), gamma)
    ret = head_groupnorm(ret, ret_gn_g).reshape(b, s, RET_WIDTH).astype(x.dtype)
    ret_out = jax.nn.silu(rg) * ret

    mixed = jnp.concatenate([fox_out.astype(x.dtype), ret_out], axis=-1)
    mix = jnp.einsum('bsc,cd->bsd', mixed, w_out)
    x = layernorm(DEEPNORM_ALPHA * x + mix, ln1_g, ln1_b)

    hid = jnp.square(jax.nn.relu(jnp.einsum('bsd,df->bsf', x, w_ff1)))
    ff = jnp.einsum('bsf,fd->bsd', hid, w_ff2)
    return layernorm(DEEPNORM_ALPHA * x + ff, ln2_g, ln2_b)


def setup_inputs(seed: int = 0) -> dict:
    key = jax.random.key(seed)
    ks = jax.random.split(key, 12)
    f32 = jnp.float32
    x = jax.random.normal(ks[0], (BATCH, SEQ, D_MODEL), f32)
    w_in = jax.random.normal(ks[1], (DEPTH, D_MODEL, IN_COLS), f32) * D_MODEL ** -0.5
    col = jnp.arange(IN_COLS)
    is_value = (((col >= SPLITS[1]) & (col < SPLITS[2]))
                | ((col >= SPLITS[5]) & (col < SPLITS[6])))
    w_in = w_in * jnp.where(is_value, DEEPNORM_BETA, 1.0).astype(f32)
    w_out = jax.random.normal(ks[2], (DEPTH, MIX_WIDTH, D_MODEL), f32) * (MIX_WIDTH ** -0.5 * DEEPNORM_BETA)
    w_ff1 = jax.random.normal(ks[3], (DEPTH, D_MODEL, D_FF), f32) * (D_MODEL ** -0.5 * DEEPNORM_BETA)
    w_ff2 = jax.random.normal(ks[4], (DEPTH, D_FF, D_MODEL), f32) * (D_FF ** -0.5 * DEEPNORM_BETA)
    ln1_g = 1.0 + 0.02 * jax.random.normal(ks[5], (DEPTH, D_MODEL), f32)
    ln1_b = 0.02 * jax.random.normal(ks[6], (DEPTH, D_MODEL), f32)
    ln2_g = 1.0 + 0.02 * jax.random.normal(ks[7], (DEPTH, D_MODEL), f32)
    ln2_b = 0.02 * jax.random.normal(ks[8], (DEPTH, D_MODEL), f32)
    b_forget = jax.random.uniform(ks[9], (DEPTH, FOX_HEADS), f32, 1.0, 5.0)
    ret_gn_g = 1.0 + 0.02 * jax.random.normal(ks[10], (DEPTH, RET_HEADS, RET_HEAD_DIM), f32)
    return {"x": x, "w_in": w_in, "w_out": w_out, "w_ff1": w_ff1, "w_ff2": w_ff2,
            "ln1_g": ln1_g, "ln1_b": ln1_b, "ln2_g": ln2_g, "ln2_b": ln2_b,
            "b_forget": b_forget, "ret_gn_g": ret_gn_g}


def reference(x, w_in, w_out, w_ff1, w_ff2, ln1_g, ln1_b, ln2_g, ln2_b, b_forget, ret_gn_g):
    gamma = 1.0 - jnp.exp2(-5.0 - jnp.arange(RET_HEADS, dtype=jnp.float32))
    for layer in range(DEPTH):
        x = hybrid_layer(x, w_in[layer], w_out[layer], w_ff1[layer], w_ff2[layer],
                         ln1_g[layer], ln1_b[layer], ln2_g[layer], ln2_b[layer],
                         b_forget[layer], ret_gn_g[layer], gamma)
    return x
```

```python
import numpy as np
import ml_dtypes
from contextlib import ExitStack

import concourse.bass as bass
import concourse.mybir as mybir
from concourse.bass_utils import run_bass_kernel_spmd

F32 = mybir.dt.float32
BF16 = mybir.dt.bfloat16
AF = mybir.ActivationFunctionType
ALU = mybir.AluOpType

D_MODEL = 1024
BATCH = 2
SEQ = 16384
DEPTH = 4
D_FF = 4096
NCORE = 8
LN_EPS = 1e-5
GN_EPS = 1e-5
ALPHA = float((2 * DEPTH) ** 0.25)
TOK_PER_CORE = BATCH * SEQ // NCORE

ENGS = ["pe", "act", "dve", "pool", "sp"]


class Op:
    __slots__ = ("eng", "fn", "deps", "signaled", "sig", "slot", "dval")

    def __init__(self, eng, fn, deps):
        self.eng = eng
        self.fn = fn
        self.deps = [d for d in deps if d is not None]
        self.signaled = False
        self.sig = 0
        self.slot = None
        self.dval = 0


class Slot:
    def __init__(self, name, sem):
        self.name = name
        self.sem = sem
        self.count = 0


class Prog:
    def __init__(self, nc, stack):
        self.nc = nc
        self.stack = stack
        self.q = {e: [] for e in ENGS}
        self.sem = {e: stack.enter_context(nc.semaphore("sem_" + e)) for e in ENGS}
        self.nslot = 0

    def slot(self, name):
        self.nslot += 1
        return Slot(name, self.stack.enter_context(self.nc.semaphore("dq_" + name)))

    def op(self, eng, fn, deps=()):
        o = Op(eng, fn, deps)
        self.q[eng].append(o)
        return o

    def dma(self, eng, slot, out, in_, deps=(), **kw):
        o = Op(eng, lambda e: e.dma_start(out=out, in_=in_, **kw), deps)
        slot.count += 16
        o.slot = slot
        o.dval = slot.count
        self.q[eng].append(o)
        return o

    def emit(self):
        for e in ENGS:
            for o in self.q[e]:
                for d in o.deps:
                    if d.slot is None and not (d.eng == "pe" and o.eng == "pe"):
                        d.signaled = True
        for e in ENGS:
            c = 0
            for o in self.q[e]:
                if o.signaled:
                    c += 1
                o.sig = c
        with self.nc.Block() as block:

            @block.tensor
            def _(e):
                self._run("pe", e)

            @block.scalar
            def _(e):
                self._run("act", e)

            @block.vector
            def _(e):
                self._run("dve", e)

            @block.gpsimd
            def _(e):
                self._run("pool", e)

            @block.sync
            def _(e):
                self._run("sp", e)

    def _run(self, name, e):
        known = {}
        for o in self.q[name]:
            waits = {}
            for d in o.deps:
                if d.slot is not None:
                    key, sem, v = d.slot.name, d.slot.sem, d.dval
                else:
                    if d.eng == "pe" and name == "pe":
                        continue
                    key, sem, v = "E" + d.eng, self.sem[d.eng], d.sig
                if v > waits.get(key, (None, 0))[1]:
                    waits[key] = (sem, v)
            for key, (sem, v) in waits.items():
                if known.get(key, 0) >= v:
                    continue
                e.wait_ge(sem, v)
                known[key] = v
            if o.fn is None:
                continue
            ins = o.fn(e)
            if o.slot is not None:
                ins.then_inc(o.slot.sem, 16)
            elif o.signaled:
                ins.then_inc(self.sem[name], 1)


def _bf(a):
    return np.ascontiguousarray(a).astype(ml_dtypes.bfloat16)


NTB = 256


def build_B():
    nc = bass.Bass("TRN2", target_bir_lowering=False)
    T = TOK_PER_CORE
    xT = nc.dram_tensor("xT", [D_MODEL, T], F32, kind="ExternalInput").ap()
    mT = nc.dram_tensor("mT", [D_MODEL, T], BF16, kind="ExternalInput").ap()
    wout = nc.dram_tensor("wout", [D_MODEL, D_MODEL], F32, kind="ExternalInput").ap()
    wff1 = nc.dram_tensor("wff1", [D_MODEL, D_FF], F32, kind="ExternalInput").ap()
    wff2 = nc.dram_tensor("wff2", [D_FF, D_MODEL], F32, kind="ExternalInput").ap()
    lnp = nc.dram_tensor("lnp", [128, 32], F32, kind="ExternalInput").ap()
    yT = nc.dram_tensor("yT", [D_MODEL, T], F32, kind="ExternalOutput").ap()
    xT3 = xT.rearrange("(kc p) t -> p kc t", p=128)
    mT3 = mT.rearrange("(kc p) t -> p kc t", p=128)
    yT3 = yT.rearrange("(kc p) t -> p kc t", p=128)
    KC = 8
    FC = 32
    with ExitStack() as st:
        P = Prog(nc, st)
        sb = lambda name, shape, dt: st.enter_context(nc.sbuf_tensor(name, shape, dt))
        ps = lambda name, shape, dt: st.enter_context(nc.psum_tensor(name, shape, dt))
        wo_sb = sb("wo_sb", [128, KC, D_MODEL], BF16)
        w1_sb = sb("w1_sb", [128, KC, D_FF], BF16)
        w2_sb = sb("w2_sb", [128, FC, D_MODEL], BF16)
        lnp_sb = sb("lnp_sb", [128, 32], F32)
        onesm = sb("onesm", [128, 128], BF16)
        xt = [sb(f"xt{i}", [128, KC, NTB], F32) for i in range(2)]
        mt = [sb(f"mt{i}", [128, KC, NTB], BF16) for i in range(2)]
        zbf = sb("zbf", [128, KC, NTB], BF16)
        zsq = sb("zsq", [128, KC, NTB], BF16)
        hid = sb("hid", [128, FC, NTB], BF16)
        mean_sb = sb("mean_sb", [128, NTB], F32)
        var_sb = sb("var_sb", [128, NTB], F32)
        rstd = sb("rstd", [128, NTB], F32)
        nmr = sb("nmr", [128, NTB], F32)
        tA = [sb(f"tA{i}", [128, NTB], F32) for i in range(2)]
        tB = [sb(f"tB{i}", [128, NTB], F32) for i in range(2)]
        rl = [sb(f"rl{i}", [128, NTB], F32) for i in range(2)]
        pA = [ps(f"pA{i}", [128, 512], F32) for i in range(2)]
        pS = [ps(f"pS{i}", [128, 512], F32) for i in range(2)]
        pF = [ps(f"pF{i}", [128, 512], F32) for i in range(2)]
        s_w = P.slot("w")
        s_c = P.slot("c")
        s_x = [P.slot("x0"), P.slot("x1")]
        s_m = [P.slot("m0"), P.slot("m1")]
        s_o = [P.slot("o0"), P.slot("o1")]

        tw = []
        t_c = P.dma("sp", s_c, lnp_sb[:, :], lnp[:, :])
        t_ones = P.op("pool", lambda e: e.memset(onesm[:, :], 1.0 / D_MODEL))
        wo3 = wout.rearrange("(kc p) n -> p kc n", p=128)
        w13 = wff1.rearrange("(kc p) n -> p kc n", p=128)
        w23 = wff2.rearrange("(kc p) n -> p kc n", p=128)
        for kc in range(KC):
            tw.append(P.dma("pool", s_w, wo_sb[:, kc, :], wo3[:, kc, :], max_dma_last_dim=4096))
        for kc in range(KC):
            tw.append(P.dma("pool", s_w, w1_sb[:, kc, :], w13[:, kc, :], max_dma_last_dim=4096))
        for kc in range(FC):
            tw.append(P.dma("pool", s_w, w2_sb[:, kc, :], w23[:, kc, :], max_dma_last_dim=4096))
        t_w = tw[-1]

        ntile = T // NTB
        last_use_x = [None, None]
        last_use_m = [None, None]
        pA_free = [None, None]
        pS_free = [None, None]
        pF_free = [None, None]
        rl_free = [None, None]
        tA_free = [None, None]
        tB_free = [None, None]
        zbf_free = [None]
        hid_free = [None]
        stat_free = [None]
        out_toks = []
        cnt = {"a": 0, "f": 0, "t": 0}

        def load(tt):
            sl = tt % 2
            c0 = tt * NTB
            a = P.dma("sp", s_x[sl], xt[sl][:, :, :], xT3[:, :, c0:c0 + NTB], deps=[last_use_x[sl]])
            b = P.dma("sp", s_m[sl], mt[sl][:, :, :], mT3[:, :, c0:c0 + NTB], deps=[last_use_m[sl]])
            return a, b

        def layer_norm(xs, goff, boff, zdeps, want_bf):
            tz, tq = [], []
            for dc in range(KC):
                tz.append(P.op("act", lambda e, dc=dc: e.activation(out=zbf[:, dc, :], in_=xs[:, dc, :], func=AF.Copy),
                               deps=[zdeps[dc], zbf_free[0]]))
                tq.append(P.op("pool", lambda e, dc=dc: e.tensor_tensor(out=zsq[:, dc, :], in0=xs[:, dc, :], in1=xs[:, dc, :], op=ALU.mult),
                               deps=[zdeps[dc], zbf_free[0]]))
            tm = None
            for dc in range(KC):
                tm = P.op("pe", lambda e, dc=dc: e.matmul(pS[0][:, 0:NTB], onesm[:, :], zbf[:, dc, :], start=(dc == 0), stop=(dc == KC - 1)),
                          deps=[tz[dc], pS_free[0], t_ones])
            ts_ = None
            for dc in range(KC):
                ts_ = P.op("pe", lambda e, dc=dc: e.matmul(pS[1][:, 0:NTB], onesm[:, :], zsq[:, dc, :], start=(dc == 0), stop=(dc == KC - 1)),
                           deps=[tq[dc], pS_free[1]])
            zbf_free[0] = ts_
            t1 = P.op("act", lambda e: e.activation(out=mean_sb[:, :], in_=pS[0][:, 0:NTB], func=AF.Copy), deps=[tm, stat_free[0]])
            pS_free[0] = t1
            t2 = P.op("dve", lambda e: e.tensor_tensor(out=var_sb[:, :], in0=mean_sb[:, :], in1=mean_sb[:, :], op=ALU.mult), deps=[t1, stat_free[0]])
            t3 = P.op("dve", lambda e: e.tensor_tensor(out=var_sb[:, :], in0=pS[1][:, 0:NTB], in1=var_sb[:, :], op=ALU.subtract), deps=[t2, ts_])
            pS_free[1] = t3
            t4 = P.op("act", lambda e: e.activation(out=var_sb[:, :], in_=var_sb[:, :], func=AF.Ln, bias=LN_EPS_AP[0], scale=1.0), deps=[t3, t_eps])
            t5 = P.op("act", lambda e: e.activation(out=rstd[:, :], in_=var_sb[:, :], func=AF.Exp, scale=-0.5), deps=[t4, stat_free[0]])
            t6 = P.op("dve", lambda e: e.scalar_tensor_tensor(out=nmr[:, :], in0=mean_sb[:, :], scalar=-1.0, in1=rstd[:, :], op0=ALU.mult, op1=ALU.mult),
                      deps=[t5, t1, stat_free[0]])
            ty, tyb = [], []
            last = None
            for dc in range(KC):
                k = cnt["t"] % 2
                cnt["t"] += 1
                a = P.op("dve", lambda e, dc=dc, k=k: e.tensor_tensor(out=tA[k][:, :], in0=xs[:, dc, :], in1=rstd[:, :], op=ALU.mult),
                         deps=[t5, zdeps[dc], tA_free[k], tz[dc], tq[dc]])
                b = P.op("pool", lambda e, k=k: e.tensor_tensor(out=tB[k][:, :], in0=tA[k][:, :], in1=nmr[:, :], op=ALU.add),
                         deps=[a, t6, tB_free[k]])
                tA_free[k] = b
                c = P.op("act", lambda e, dc=dc, k=k: e.activation(out=xs[:, dc, :], in_=tB[k][:, :], func=AF.Identity,
                                                                    bias=lnp_sb[:, boff + dc:boff + dc + 1], scale=lnp_sb[:, goff + dc:goff + dc + 1]),
                         deps=[b, a, t_c])
                tB_free[k] = c
                ty.append(c)
                last = c
                if want_bf:
                    d = P.op("pool", lambda e, dc=dc: e.tensor_copy(out=zbf[:, dc, :], in_=xs[:, dc, :]), deps=[c, zbf_free[0]])
                    tyb.append(d)
                    last = d
            stat_free[0] = last
            return ty, tyb

        LN_EPS_AP = [None]
        epsb = sb("epsb", [128, 1], F32)
        t_eps = P.op("pool", lambda e: e.memset(epsb[:, :], LN_EPS))
        LN_EPS_AP[0] = epsb[:, 0:1]

        def tile(tt, tx, tmm):
            sl = tt % 2
            xs = xt[sl]
            zd = []
            for dc in range(KC):
                k = cnt["a"] % 2
                cnt["a"] += 1
                tl = None
                for kc in range(KC):
                    tl = P.op("pe", lambda e, dc=dc, kc=kc, k=k: e.matmul(pA[k][:, 0:NTB], wo_sb[:, kc, dc * 128:(dc + 1) * 128], mt[sl][:, kc, :],
                                                                         start=(kc == 0), stop=(kc == KC - 1)),
                              deps=[t_w, tmm, pA_free[k]])
                z = P.op("dve", lambda e, dc=dc, k=k: e.scalar_tensor_tensor(out=xs[:, dc, :], in0=xs[:, dc, :], scalar=ALPHA, in1=pA[k][:, 0:NTB],
                                                                             op0=ALU.mult, op1=ALU.add),
                         deps=[tl, tx, t_eps])
                pA_free[k] = z
                zd.append(z)
            last_use_m[sl] = tl
            ty, tyb = layer_norm(xs, 0, 8, zd, True)
            th = []
            for fc in range(FC):
                k = cnt["f"] % 2
                cnt["f"] += 1
                tl = None
                for kc in range(KC):
                    tl = P.op("pe", lambda e, fc=fc, kc=kc, k=k: e.matmul(pF[k][:, 0:NTB], w1_sb[:, kc, fc * 128:(fc + 1) * 128], zbf[:, kc, :],
                                                                         start=(kc == 0), stop=(kc == KC - 1)),
                              deps=[tyb[kc], pF_free[k]])
                r = P.op("act", lambda e, k=k: e.activation(out=rl[k][:, :], in_=pF[k][:, 0:NTB], func=AF.Relu), deps=[tl, rl_free[k]])
                h = P.op("dve", lambda e, fc=fc, k=k: e.tensor_tensor(out=hid[:, fc, :], in0=pF[k][:, 0:NTB], in1=rl[k][:, :], op=ALU.mult),
                         deps=[r, hid_free[0]])
                pF_free[k] = h
                rl_free[k] = h
                th.append(h)
            zbf_free[0] = tl
            zd = []
            for dc in range(KC):
                k = cnt["a"] % 2
                cnt["a"] += 1
                tl = None
                for fc in range(FC):
                    tl = P.op("pe", lambda e, dc=dc, fc=fc, k=k: e.matmul(pA[k][:, 0:NTB], w2_sb[:, fc, dc * 128:(dc + 1) * 128], hid[:, fc, :],
                                                                         start=(fc == 0), stop=(fc == FC - 1)),
                              deps=[th[fc], pA_free[k]])
                z = P.op("dve", lambda e, dc=dc, k=k: e.scalar_tensor_tensor(out=xs[:, dc, :], in0=xs[:, dc, :], scalar=ALPHA, in1=pA[k][:, 0:NTB],
                                                                             op0=ALU.mult, op1=ALU.add),
                         deps=[tl, ty[dc]])
                pA_free[k] = z
                zd.append(z)
            hid_free[0] = tl
            ty2, _ = layer_norm(xs, 16, 24, zd, False)
            c0 = tt * NTB
            o = P.dma("sp", s_o[sl], yT3[:, :, c0:c0 + NTB], xs[:, :, :], deps=ty2)
            out_toks.append(o)
            last_use_x[sl] = o
        pend = load(0)
        for tt in range(ntile):
            tx, tmm = pend
            if tt + 1 < ntile:
                pend = load(tt + 1)
            tile(tt, tx, tmm)
        P.op("sp", None, deps=out_toks)
        P.emit()
    return nc


TQ = 512
NTA = SEQ // TQ
NBLK = SEQ // 128
C_Q = (0, 64)
C_K = (128, 192)
C_RQ, C_RQS, C_RK, C_RKS, C_TM = 256, 384, 512, 640, 768
NCOLA = 768 + 386


def _flat(deps):
    out = []
    for d in deps:
        if d is None:
            continue
        if isinstance(d, (list, tuple)):
            out.extend(_flat(d))
        else:
            out.append(d)
    return out


def build_A(ntiles=NTA, parts=("ret", "attn")):
    nc = bass.Bass("TRN2", target_bir_lowering=False)
    xTb = nc.dram_tensor("xTb", [D_MODEL, SEQ], F32, kind="ExternalInput").ap()
    wa = nc.dram_tensor("wa", [D_MODEL, NCOLA], F32, kind="ExternalInput").ap()
    tabs = nc.dram_tensor("tabs", [128, 4, SEQ], F32, kind="ExternalInput").ap()
    cst = nc.dram_tensor("cst", [128, 3, 128], F32, kind="ExternalInput").ap()
    gng = nc.dram_tensor("gng", [128, 128], F32, kind="ExternalInput").ap()
    small = nc.dram_tensor("small", [128, 8], F32, kind="ExternalInput").ap()
    mo = nc.dram_tensor("mo", [256, SEQ], BF16, kind="ExternalOutput").ap()
    xT3 = xTb.rearrange("(kc p) t -> p kc t", p=128)
    wa3 = wa.rearrange("(kc p) n -> p kc n", p=128)
    KC = 8
    with ExitStack() as st:
        P = Prog(nc, st)
        sb = lambda name, shape, dt: st.enter_context(nc.sbuf_tensor(name, shape, dt))
        ps = lambda name, shape, dt: st.enter_context(nc.psum_tensor(name, shape, dt))
        op = lambda eng, fn, deps=(): P.op(eng, fn, _flat(deps))
        dma = lambda eng, slot, out, in_, deps=(), **kw: P.dma(eng, slot, out, in_, _flat(deps), **kw)

        wa_sb = sb("wa_sb", [128, KC, NCOLA], BF16)
        KT = [sb(f"KT{h}", [67, SEQ], BF16) for h in range(2)]
        VA = sb("VA", [128, 2, NBLK, 65], BF16)
        AC = sb("AC", [128, 2, NBLK], F32)
        CARRY = sb("CARRY", [128, 2, NTA + 1], F32)
        xt = [sb(f"xt{i}", [128, KC, TQ], BF16) for i in range(2)]
        tb = [sb(f"tb{i}", [128, 4, TQ], F32) for i in range(2)]
        QT = [[sb(f"QT{h}_{s}", [67, TQ], BF16) for s in range(2)] for h in range(2)]
        qr = sb("qr", [128, TQ], BF16)
        kr = sb("kr", [128, TQ], BF16)
        rt1 = sb("rt1", [128, TQ], F32)
        rt2 = sb("rt2", [128, TQ], F32)
        rv_sb = sb("rv_sb", [128, 4, 128], BF16)
        G = sb("G", [128, 4, 128], F32)
        sg1 = sb("sg1", [128, 128], F32)
        sg2 = sb("sg2", [128, 128], F32)
        zt = sb("zt", [128, 4, 2], F32)
        lf = sb("lf", [128, 4, 2], F32)
        AL = sb("AL", [128, 5, 2], F32)
        ar_sb = sb("ar_sb", [2, TQ], F32)
        r1 = sb("r1", [2, TQ], F32)
        r2 = sb("r2", [2, TQ], F32)
        PC = sb("PC", [2, 3, TQ], BF16)
        PT = [sb(f"PT{i}", [128, TQ], BF16) for i in range(3)]
        BIAS = [sb(f"BIAS{h}", [128, NBLK], F32) for h in range(2)]
        R = sb("R", [65, TQ], F32)
        RB = sb("RB", [64, TQ], F32)
        MO = [sb(f"MO{h}", [64, TQ], BF16) for h in range(2)]
        inT = sb("inT", [128, 128], BF16)
        kt_sb = sb("kt_sb", [128, 128], BF16)
        W = sb("W", [128, 128], F32)
        st_bf = sb("st_bf", [128, 128], BF16)
        st6 = sb("st6", [128, 6], F32)
        mv = sb("mv", [128, 2], F32)
        lnv = sb("lnv", [128, 1], F32)
        rs = sb("rs", [128, 1], F32)
        nrm = sb("nrm", [128, 128], F32)
        y_bf = sb("y_bf", [128, 128], BF16)
        YT = sb("YT", [128, TQ], BF16)
        cst_sb = sb("cst_sb", [128, 3, 128], F32)
        tri_bf = sb("tri_bf", [128, 128], BF16)
        id_bf = sb("id_bf", [128, 128], BF16)
        gng_sb = sb("gng_sb", [128, 128], F32)
        sm = sb("sm", [128, 8], F32)
        cv = sb("cv", [128, 2], F32)
        tmb = [sb(f"tmb{i}", [128, 386], F32) for i in range(2)]

        S = [ps(f"S{i}", [128, 512], F32) for i in range(2)]
        O = [ps(f"O{i}", [128, 512], F32) for i in range(2)]
        PJ = [ps(f"PJ{i}", [128, 512], F32) for i in range(2)]
        M0 = ps("M0", [128, 512], F32)
        M1 = ps("M1", [128, 1024], BF16)

        tri_f = cst_sb[:, 0, :]
        ones_f = cst_sb[:, 1, :]

        s_w = P.slot("w")
        s_c = P.slot("c")
        s_x = [P.slot("x0"), P.slot("x1")]
        s_t = [P.slot("t0"), P.slot("t1")]
        s_a = [[P.slot("a00"), P.slot("a01")], [P.slot("a10"), P.slot("a11")]]
        s_oh = [P.slot("oh0"), P.slot("oh1")]
        s_oy = P.slot("oy")

        t_c = [dma("sp", s_c, cst_sb[:, :, :], cst[:, :, :]),
               dma("sp", s_c, gng_sb[:, :], gng[:, :]),
               dma("sp", s_c, sm[:, :], small[:, :])]
        t_c = t_c[-1:] + t_c[:-1]
        t_w = None
        for kc in range(KC):
            t_w = dma("pool", s_w, wa_sb[:, kc, :], wa3[:, kc, :], max_dma_last_dim=4096)
        t_i = []
        t_i.append(op("dve", lambda e: e.tensor_copy(out=tri_bf[:, :], in_=cst_sb[:, 0, :]), [t_c]))
        t_i.append(op("dve", lambda e: e.tensor_copy(out=id_bf[:, :], in_=cst_sb[:, 2, :]), [t_c]))
        t_i.append(op("pool", lambda e: e.memset(KT[0][64:67, :], 1.0)))
        t_i.append(op("pool", lambda e: e.memset(KT[1][64:67, :], 1.0)))
        t_i.append(op("pool", lambda e: e.memset(VA[:, :, :, 64:65], 1.0)))
        t_i.append(op("pool", lambda e: e.memset(CARRY[:, :, 0:1], 0.0)))
        t_i.append(op("pool", lambda e: e.memset(AL[:, 0:1, :], 0.0)))
        t_i.append(op("pool", lambda e: e.memset(W[:, :], 0.0)))
        t_i.append(op("pool", lambda e: e.memset(st_bf[:, :], 0.0)))
        t_i.append(op("pool", lambda e: e.memset(cv[:, 0:1], 1.0)))
        t_i.append(op("pool", lambda e: e.memset(cv[:, 1:2], GN_EPS)))
        t_init = t_i + t_c

        state = dict(
            pj=0, PJ_free=[None, None], xt_free=[None, None], tb_free=[None, None],
            QT_free=[[None, None], [None, None]], qr_free=None, kr_free=None,
            rt_free=None, rv_free=None, G_free=None, sg_free=None, zt_free=None, lf_free=None,
            AL_free=None, ar_free=None, PC_free=None, M0c_free=None,
            S_free=[None, None], scnt=0, PT_free=[None, None, None], pcnt=0,
            O_free=[None, None], BIAS_free=[None, None], R_free=None, RB_free=None, MO_free=[None, None],
            inT_free=None, kts_free=None, Wst=t_i, stbf=None, stbf_free=None,
            nrm_free=None, ybf_free=None, YT_free=None, M0r_free=[None, None, None], M1_free=[None, None],
            mv_free=None, tm_free=[None, None], MC_free=None, MT_free=None,
        )
        kt_tok = [[None] * NTA for _ in range(2)]
        va_tok = [None] * NTA
        qt_tok = [[None, None], [None, None]]
        ac_tok = [None] * NTA
        ca_tok = [None] * (NTA + 1)
        ca_tok[0] = t_i
        ret_in = [None] * NTA
        out_toks = []

        def pj_get():
            k = state["pj"] % 2
            state["pj"] += 1
            return k, state["PJ_free"][k]

        def load(i):
            sl = i % 2
            c0 = i * TQ
            a = []
            for q4 in range(4):
                a.append(dma("pool", s_x[sl], xt[sl][:, 2 * q4:2 * q4 + 2, :], xT3[:, 2 * q4:2 * q4 + 2, c0:c0 + TQ], [state["xt_free"][sl]], max_dma_last_dim=2048))
            b = dma("sp", s_t[sl], tb[sl][:, :, :], tabs[:, :, c0:c0 + TQ], [state["tb_free"][sl]])
            return a, b

        def stage(i, tx, ttab):
            sl = i % 2
            c0 = i * TQ
            xs = xt[sl]
            tbs = tb[sl]
            last_pe = None
            qt_list = [[], []]
            for h in range(2):
                for which in ("q", "k"):
                    col = C_Q[h] if which == "q" else C_K[h]
                    k, fr = pj_get()
                    tl = None
                    for kc in range(KC):
                        tl = op("pe", lambda e, kc=kc, k=k, col=col: e.matmul(PJ[k][0:64, 0:TQ], wa_sb[:, kc, col:col + 64], xs[:, kc, :],
                                                                           start=(kc == 0), stop=(kc == KC - 1)),
                                [t_w, tx, fr])
                    if which == "q":
                        ev = op("act", lambda e, k=k, h=h: e.activation(out=QT[h][sl][0:64, :], in_=PJ[k][0:64, 0:TQ], func=AF.Copy, scale=0.125),
                                [tl, state["QT_free"][h][sl]])
                        qt_list[h].append(ev)
                    else:
                        ev = op("dve", lambda e, k=k, h=h: e.tensor_copy(out=KT[h][0:64, c0:c0 + TQ], in_=PJ[k][0:64, 0:TQ]), [tl])
                        kt_tok[h][i] = ev
                    state["PJ_free"][k] = ev
                    last_pe = tl
            for which in ("q", "k"):
                ca, cb = (C_RQ, C_RQS) if which == "q" else (C_RK, C_RKS)
                dst = qr if which == "q" else kr
                ti = 0 if which == "q" else 2
                ka, fra = pj_get()
                tla = None
                for kc in range(KC):
                    tla = op("pe", lambda e, kc=kc, ka=ka, ca=ca: e.matmul(PJ[ka][:, 0:TQ], wa_sb[:, kc, ca:ca + 128], xs[:, kc, :],
                                                                       start=(kc == 0), stop=(kc == KC - 1)), [t_w, tx, fra])
                kb, frb = pj_get()
                tlb = None
                for kc in range(KC):
                    tlb = op("pe", lambda e, kc=kc, kb=kb, cb=cb: e.matmul(PJ[kb][:, 0:TQ], wa_sb[:, kc, cb:cb + 128], xs[:, kc, :],
                                                                       start=(kc == 0), stop=(kc == KC - 1)), [t_w, tx, frb])
                a = op("dve", lambda e, ka=ka, ti=ti: e.tensor_tensor(out=rt1[:, :], in0=PJ[ka][:, 0:TQ], in1=tbs[:, ti, :], op=ALU.mult),
                       [tla, ttab, state["rt_free"]])
                b = op("dve", lambda e, kb=kb, ti=ti: e.tensor_tensor(out=rt2[:, :], in0=PJ[kb][:, 0:TQ], in1=tbs[:, ti + 1, :], op=ALU.mult),
                       [tlb, ttab, state["rt_free"]])
                state["PJ_free"][ka] = a
                state["PJ_free"][kb] = b
                c = op("pool", lambda e, dst=dst: e.tensor_tensor(out=dst[:, :], in0=rt1[:, :], in1=rt2[:, :], op=ALU.add),
                       [a, b, state["qr_free"] if which == "q" else state["kr_free"]])
                state["rt_free"] = c
                if which == "q":
                    tq_ready = c
                else:
                    tk_ready = c
                last_pe = tlb
            state["tb_free"][sl] = b
            tv_list = []
            trv = []
            tG = []
            tz = []
            for m in range(4):
                k, fr = pj_get()
                tl = None
                for kc in range(KC):
                    tl = op("pe", lambda e, kc=kc, k=k, m=m: e.matmul(PJ[k][:, 0:386], xs[:, kc, m * 128:(m + 1) * 128], wa_sb[:, kc, C_TM:C_TM + 386],
                                                                     start=(kc == 0), stop=(kc == KC - 1)), [t_w, tx, fr])
                blk = 4 * i + m
                tm = tmb[m % 2]
                e0 = op("act", lambda e, k=k, tm=tm: e.activation(out=tm[:, :], in_=PJ[k][:, 0:386], func=AF.Copy), [tl, state["tm_free"][m % 2]])
                state["PJ_free"][k] = e0
                e1 = op("pool", lambda e, tm=tm, blk=blk: e.tensor_copy(out=VA[:, :, blk, 0:64], in_=tm[:, 0:128].rearrange("p (h d) -> p h d", h=2)),
                        [e0, t_init])
                tv_list.append(e1)
                e2 = op("pool", lambda e, tm=tm, m=m: e.tensor_copy(out=rv_sb[:, m, :], in_=tm[:, 128:256]), [e0, state["rv_free"]])
                trv.append(e2)
                e3 = op("act", lambda e, tm=tm: e.activation(out=sg1[:, :], in_=tm[:, 256:384], func=AF.Exp, scale=-1.0), [e0, state["sg_free"]])
                e4 = op("dve", lambda e: e.tensor_scalar_add(sg1[:, :], sg1[:, :], 1.0), [e3])
                e5 = op("dve", lambda e: e.reciprocal(out=sg1[:, :], in_=sg1[:, :]), [e4])
                e6 = op("dve", lambda e, tm=tm: e.tensor_tensor(out=sg2[:, :], in0=tm[:, 256:384], in1=sg1[:, :], op=ALU.mult), [e5])
                e7 = op("pool", lambda e, m=m: e.tensor_tensor(out=G[:, m, :], in0=sg2[:, :], in1=gng_sb[:, :], op=ALU.mult), [e6, state["G_free"], t_init])
                state["sg_free"] = e7
                tG.append(e7)
                e8 = op("dve", lambda e, tm=tm, m=m: e.tensor_tensor(out=zt[:, m, :], in0=tm[:, 384:386], in1=sm[:, 0:2], op=ALU.add),
                        [e0, state["zt_free"], t_init])
                tz.append(e8)
                state["tm_free"][m % 2] = [e1, e2, e3, e6, e8]
                last_pe = tl
            state["xt_free"][sl] = last_pe
            va_tok[i] = tv_list
            f1 = op("act", lambda e: e.activation(out=zt[:, :, :], in_=zt[:, :, :], func=AF.Exp, scale=-1.0), [tz])
            f2 = op("act", lambda e: e.activation(out=lf[:, :, :], in_=zt[:, :, :], func=AF.Ln, bias=cv[:, 0:1], scale=1.0), [f1, state["lf_free"], t_init])
            state["zt_free"] = f2
            g1 = op("dve", lambda e: e.tensor_copy(out=AL[:, 1, :], in_=lf[:, 0, :]), [f2, state["AL_free"], t_init])
            g2 = op("dve", lambda e: e.tensor_tensor(out=AL[:, 2, :], in0=AL[:, 1, :], in1=lf[:, 1, :], op=ALU.add), [g1])
            g3 = op("dve", lambda e: e.tensor_tensor(out=AL[:, 3, :], in0=AL[:, 2, :], in1=lf[:, 2, :], op=ALU.add), [g2])
            g4 = op("dve", lambda e: e.tensor_tensor(out=AL[:, 4, :], in0=AL[:, 3, :], in1=lf[:, 3, :], op=ALU.add), [g3])
            lf8 = lf[:, :, :].rearrange("p m h -> p (m h)")
            al8 = AL[:, 0:4, :].rearrange("p m h -> p (m h)")
            h1 = op("pe", lambda e: e.matmul(M0[:, 384:392], tri_f, lf8, start=True, stop=False), [f2, state["MC_free"], t_init])
            h2 = op("pe", lambda e: e.matmul(M0[:, 384:392], ones_f, al8, start=False, stop=True), [g3])
            h3 = op("pe", lambda e: e.matmul(M0[:, 392:394], ones_f, AL[:, 4, :], start=True, stop=True), [g4])
            psc = M0[:, 384:392].rearrange("p (m h) -> p m h", h=2)
            acs = []
            for h in range(2):
                acs.append(op("dve", lambda e, h=h: e.tensor_scalar(out=AC[:, h, 4 * i:4 * i + 4], in0=psc[:, :, h], scalar1=CARRY[:, h, i:i + 1], scalar2=0.0,
                                                                    op0=ALU.add, op1=ALU.add), [h2, h3, ca_tok[i]]))
            cn = op("dve", lambda e: e.tensor_tensor(out=CARRY[:, :, i + 1], in0=M0[:, 392:394], in1=CARRY[:, :, i], op=ALU.add), [h3, ca_tok[i]])
            ac_tok[i] = acs
            ca_tok[i + 1] = cn
            state["MC_free"] = acs + [cn]
            k, fr = pj_get()
            tl = None
            for m in range(4):
                op("pe", lambda e, k=k, m=m: e.matmul(PJ[k][0:2, m * 128:(m + 1) * 128], lf[:, m, :], tri_f, start=True, stop=False), [f2, fr, t_init])
                tl = op("pe", lambda e, k=k, m=m: e.matmul(PJ[k][0:2, m * 128:(m + 1) * 128], AL[:, m, :], ones_f, start=False, stop=True), [g3])
            state["lf_free"] = tl
            state["AL_free"] = [tl, h3]
            s1 = op("act", lambda e, k=k: e.activation(out=PC[:, 0, :], in_=PJ[k][0:2, 0:TQ], func=AF.Copy, scale=-1.0), [tl, state["PC_free"]])
            s2 = op("dve", lambda e, k=k: e.scalar_tensor_tensor(out=r1[:, :], in0=PJ[k][0:2, 0:TQ], scalar=-1.0, in1=PC[:, 0, :], op0=ALU.mult, op1=ALU.subtract),
                    [s1, state["ar_free"]])
            state["PJ_free"][k] = s2
            s3 = op("act", lambda e: e.activation(out=PC[:, 1, :], in_=r1[:, :], func=AF.Copy), [s2])
            s4 = op("dve", lambda e: e.tensor_tensor(out=r2[:, :], in0=r1[:, :], in1=PC[:, 1, :], op=ALU.subtract), [s3])
            s5 = op("act", lambda e: e.activation(out=PC[:, 2, :], in_=r2[:, :], func=AF.Copy), [s4])
            state["ar_free"] = s5
            dl = []
            for h in range(2):
                for r in range(3):
                    dl.append(dma("sp", s_a[h][sl], QT[h][sl][64 + r:65 + r, :], PC[h:h + 1, r, :], [s1, s3, s5, state["QT_free"][h][sl]]))
                    qt_list[h].append(dl[-1])
            state["PC_free"] = dl
            qt_tok[0][sl] = qt_list[0]
            qt_tok[1][sl] = qt_list[1]
            ret_in[i] = (tq_ready, tk_ready, trv, tG)

        def retention(i):
            sl = i % 2
            c0 = i * TQ
            tq_ready, tk_ready, trv, tG = ret_in[i]
            ycp = []
            for m in range(4):
                cs = slice(m * 128, (m + 1) * 128)
                kin, frin = pj_get()
                a1 = op("pe", lambda e, cs=cs, kin=kin: e.matmul(PJ[kin][:, 0:128], kr[:, cs], qr[:, cs], start=True, stop=True), [tq_ready, tk_ready, frin])
                a2 = op("dve", lambda e, kin=kin: e.tensor_tensor(out=inT[:, :], in0=PJ[kin][:, 0:128], in1=tri_f, op=ALU.mult), [a1, state["inT_free"], t_init])
                state["PJ_free"][kin] = a2
                a3 = op("pe", lambda e, cs=cs: e.transpose(M1[:, 0:128], kr[:, cs], id_bf[:, :]), [tk_ready, state["MT_free"], t_init])
                a4 = op("act", lambda e: e.activation(out=kt_sb[:, :], in_=M1[:, 0:128], func=AF.Copy), [a3, state["kts_free"]])
                state["MT_free"] = a4
                op("pe", lambda e, m=m: e.matmul(M0[:, 128:256], inT[:, :], rv_sb[:, m, :], start=True, stop=False), [a2, trv[m], state["MC_free"]])
                a5 = op("pe", lambda e, cs=cs: e.matmul(M0[:, 128:256], qr[:, cs], st_bf[:, :], start=False, stop=True), [state["stbf"], t_init])
                state["inT_free"] = a5
                kkv, frkv = pj_get()
                a6 = op("pe", lambda e, m=m, kkv=kkv: e.matmul(PJ[kkv][:, 0:128], kt_sb[:, :], rv_sb[:, m, :], start=True, stop=True), [a4, frkv])
                state["kts_free"] = a6
                a7 = op("dve", lambda e, kkv=kkv: e.scalar_tensor_tensor(out=W[:, :], in0=W[:, :], scalar=sm[:, 2:3], in1=PJ[kkv][:, 0:128], op0=ALU.mult, op1=ALU.add),
                        [a6, state["Wst"], t_init])
                state["PJ_free"][kkv] = a7
                a8 = op("act", lambda e: e.activation(out=st_bf[:, :], in_=W[:, :], func=AF.Identity, scale=sm[:, 2:3]), [a7, a5])
                state["Wst"] = a8
                state["stbf"] = a8
                b1 = op("dve", lambda e: e.bn_stats(out=st6[:, :], in_=M0[:, 128:256]), [a5, state["mv_free"]])
                b2 = op("dve", lambda e: e.bn_aggr(out=mv[:, :], in_=st6[:, :]), [b1])
                b3 = op("act", lambda e: e.activation(out=lnv[:, :], in_=mv[:, 1:2], func=AF.Ln, bias=cv[:, 1:2], scale=1.0), [b2, t_init])
                b4 = op("act", lambda e: e.activation(out=rs[:, :], in_=lnv[:, :], func=AF.Exp, scale=-0.5), [b3])
                b5 = op("dve", lambda e: e.tensor_scalar(out=nrm[:, :], in0=M0[:, 128:256], scalar1=mv[:, 0:1], scalar2=rs[:, 0:1], op0=ALU.subtract, op1=ALU.mult),
                        [b4, b2, state["nrm_free"]])
                state["MC_free"] = b5
                state["mv_free"] = b5
                b6 = op("pool", lambda e, m=m: e.tensor_tensor(out=y_bf[:, :], in0=nrm[:, :], in1=G[:, m, :], op=ALU.mult), [b5, tG[m], state["ybf_free"]])
                state["nrm_free"] = b6
                b7 = op("pe", lambda e: e.transpose(M1[:, 128:256], y_bf[:, :], id_bf[:, :]), [b6, state["MT_free"], t_init])
                state["ybf_free"] = b7
                b8 = op("act", lambda e, cs=cs: e.activation(out=YT[:, cs], in_=M1[:, 128:256], func=AF.Copy), [b7, state["YT_free"]])
                state["MT_free"] = b8
                ycp.append(b8)
                last_pe = a6
            state["qr_free"] = a5
            state["kr_free"] = a3
            state["rv_free"] = a6
            state["G_free"] = b6
            d = dma("sp", s_oy, mo[128:256, c0:c0 + TQ], YT[:, :], [ycp])
            state["YT_free"] = d
            out_toks.append(d)

        def attention(i):
            sl = i % 2
            c0t = i * TQ
            nj = 4 * i + 4
            for h in range(2):
                tbias = op("dve", lambda e, h=h: e.tensor_scalar(out=BIAS[h][:, 0:nj], in0=AC[:, h, 0:nj], scalar1=CARRY[:, h, i:i + 1], scalar2=0.0,
                                                                 op0=ALU.subtract, op1=ALU.add),
                           [ac_tok[i], ca_tok[i], state["BIAS_free"][h]])
                qk = {}

                def issue_qk(j, h=h):
                    m = j - 4 * i
                    cc = 128 * m if m > 0 else 0
                    N = TQ - cc
                    sbk = state["scnt"] % 2
                    state["scnt"] += 1
                    t = op("pe", lambda e, j=j, sbk=sbk, cc=cc, N=N: e.matmul(S[sbk][:, 0:N], KT[h][0:67, j * 128:(j + 1) * 128], QT[h][sl][0:67, cc:TQ],
                                                                             start=True, stop=True),
                           [state["S_free"][sbk], kt_tok[h][j // 4], qt_tok[h][sl], t_init])
                    qk[j] = (t, sbk, cc, N)

                issue_qk(0)
                if nj > 1:
                    issue_qk(1)
                tv = None
                for j in range(nj):
                    t, sbk, cc, N = qk[j]
                    pb = state["pcnt"] % 3
                    state["pcnt"] += 1
                    te = op("act", lambda e, j=j, sbk=sbk, N=N, pb=pb, h=h: e.activation(out=PT[pb][:, 0:N], in_=S[sbk][:, 0:N], func=AF.Exp,
                                                                                   bias=BIAS[h][:, j:j + 1], scale=1.0),
                            [t, tbias, state["PT_free"][pb]])
                    state["S_free"][sbk] = te
                    tp = te
                    if j >= 4 * i:
                        tp = op("pool", lambda e, pb=pb: e.tensor_tensor(out=PT[pb][:, 0:128], in0=PT[pb][:, 0:128], in1=tri_bf[:, :], op=ALU.mult), [te, t_init])
                    tv = op("pe", lambda e, j=j, pb=pb, cc=cc, N=N, h=h: e.matmul(O[h][0:65, cc:TQ], VA[:, h, j, 0:65], PT[pb][:, 0:N],
                                                                           start=(j == 0), stop=(j == nj - 1)),
                            [tp, va_tok[j // 4], state["O_free"][h] if j == 0 else None, t_init])
                    state["PT_free"][pb] = tv
                    if j + 2 < nj:
                        issue_qk(j + 2)
                state["QT_free"][h][sl] = tv
                state["BIAS_free"][h] = tv
                n1 = op("dve", lambda e, h=h: e.reciprocal(out=R[64:65, :], in_=O[h][64:65, 0:TQ]), [tv, state["R_free"]])
                k, fr = pj_get()
                n2 = op("pe", lambda e, k=k: e.matmul(PJ[k][0:64, 0:TQ], cst_sb[64:65, 1, 0:64], R[64:65, :], start=True, stop=True), [n1, fr, t_init])
                state["R_free"] = n2
                n3 = op("act", lambda e, k=k: e.activation(out=RB[:, :], in_=PJ[k][0:64, 0:TQ], func=AF.Copy), [n2, state["RB_free"]])
                state["PJ_free"][k] = n3
                n4 = op("dve", lambda e, h=h: e.tensor_tensor(out=MO[h][:, :], in0=O[h][0:64, 0:TQ], in1=RB[:, :], op=ALU.mult), [n3, state["MO_free"][h]])
                state["O_free"][h] = n4
                state["RB_free"] = n4
                d = dma("sp", s_oh[h], mo[h * 64:(h + 1) * 64, c0t:c0t + TQ], MO[h][:, :], [n4])
                state["MO_free"][h] = d
                out_toks.append(d)

        pend = load(0)
        for i in range(ntiles):
            tx, ttab = pend
            stage(i, tx, ttab)
            if i + 1 < ntiles:
                pend = load(i + 1)
            if "ret" in parts:
                retention(i)
            if "attn" in parts:
                attention(i)
        op("sp", None, [out_toks])
        P.emit()
    return nc


FOXW = 512
SP3 = 3 * FOXW + 8


def _const_tables(g):
    f32 = np.float32
    gamma = f32(1.0) - np.exp2(f32(-5.0 - g)).astype(f32)
    log_g = np.log(gamma).astype(f32)
    idx = np.arange(128, dtype=f32)
    xi = np.exp(log_g * (idx + f32(1.0))).astype(np.float64)
    half = 64
    inv_freq = (f32(10000.0) ** (-np.arange(half, dtype=f32) / f32(half))).astype(f32)
    ang = (np.arange(SEQ, dtype=f32)[:, None] * inv_freq[None, :]).astype(f32)
    cos = np.cos(ang.astype(np.float64)).T
    sin = np.sin(ang.astype(np.float64)).T
    C = np.concatenate([cos, cos], axis=0)
    Sg = np.concatenate([-sin, sin], axis=0)
    xi_t = np.tile(xi, SEQ // 128)[None, :]
    ks = 128.0 ** -0.5
    tabs = np.stack([C * xi_t, Sg * xi_t, C / xi_t * ks, Sg / xi_t * ks], axis=1)
    decay = np.exp(np.float64(log_g) * 128.0)
    return np.ascontiguousarray(tabs.astype(f32)), f32(decay)


def _cst():
    tri = np.triu(np.ones((128, 128), np.float32))
    ones = np.ones((128, 128), np.float32)
    ident = np.eye(128, dtype=np.float32)
    return np.ascontiguousarray(np.stack([tri, ones, ident], axis=1))


def _wa_cols(g):
    h0, h1 = 2 * g, 2 * g + 1
    cols = []
    cols += list(range(h0 * 64, h0 * 64 + 64)) + list(range(h1 * 64, h1 * 64 + 64))
    cols += list(range(FOXW + h0 * 64, FOXW + h0 * 64 + 64)) + list(range(FOXW + h1 * 64, FOXW + h1 * 64 + 64))
    rq = SP3 + g * 128
    rk = SP3 + 512 + g * 128
    rv = SP3 + 1024 + g * 128
    rg = SP3 + 1536 + g * 128
    cols += list(range(rq, rq + 128))
    cols += list(range(rq + 64, rq + 128)) + list(range(rq, rq + 64))
    cols += list(range(rk, rk + 128))
    cols += list(range(rk + 64, rk + 128)) + list(range(rk, rk + 64))
    cols += list(range(2 * FOXW + g * 128, 2 * FOXW + g * 128 + 128))
    cols += list(range(rv, rv + 128))
    cols += list(range(rg, rg + 128))
    cols += [3 * FOXW + h0, 3 * FOXW + h1]
    assert len(cols) == NCOLA
    return np.array(cols)


_CACHE = {}


def _get(name, fn):
    if name not in _CACHE:
        _CACHE[name] = fn()
    return _CACHE[name]


def a_inputs(xT_b, w_in_l, b_forget_l, ret_gn_g_l):
    cst = _get("cst", _cst)
    maps = []
    for c in range(NCORE):
        b, g = c // 4, c % 4
        tabs, decay = _get(("tabs", g), lambda g=g: _const_tables(g))
        small = np.zeros((128, 8), np.float32)
        small[:, 0] = b_forget_l[2 * g]
        small[:, 1] = b_forget_l[2 * g + 1]
        small[:, 2] = decay
        gng = np.ascontiguousarray(np.broadcast_to(ret_gn_g_l[g][None, :], (128, 128))).astype(np.float32)
        maps.append({"xTb": xT_b[b], "wa": np.ascontiguousarray(w_in_l[:, _wa_cols(g)]), "tabs": tabs, "cst": cst,
                     "gng": gng, "small": small})
    return maps


def a_gather(results):
    mixT = [np.empty((D_MODEL, SEQ), ml_dtypes.bfloat16) for _ in range(BATCH)]
    for c in range(NCORE):
        b, g = c // 4, c % 4
        mo = results[c]["mo"]
        mixT[b][g * 128:(g + 1) * 128] = mo[0:128]
        mixT[b][512 + g * 128:512 + (g + 1) * 128] = mo[128:256]
    return mixT


def b_inputs(xT_b, mixT, w_out_l, w_ff1_l, w_ff2_l, g1, b1, g2, b2):
    lnp = np.zeros((128, 32), np.float32)
    for k, v in enumerate((g1, b1, g2, b2)):
        lnp[:, k * 8:(k + 1) * 8] = v.reshape(8, 128).T
    maps = []
    T = TOK_PER_CORE
    for c in range(NCORE):
        b, q = c // 4, c % 4
        maps.append({"xT": np.ascontiguousarray(xT_b[b][:, q * T:(q + 1) * T]),
                     "mT": np.ascontiguousarray(mixT[b][:, q * T:(q + 1) * T]),
                     "wout": w_out_l, "wff1": w_ff1_l, "wff2": w_ff2_l, "lnp": lnp})
    return maps


def b_gather(results):
    T = TOK_PER_CORE
    xT_b = [np.empty((D_MODEL, SEQ), np.float32) for _ in range(BATCH)]
    for c in range(NCORE):
        b, q = c // 4, c % 4
        xT_b[b][:, q * T:(q + 1) * T] = results[c]["yT"]
    return xT_b


def kernel(x, w_in, w_out, w_ff1, w_ff2, ln1_g, ln1_b, ln2_g, ln2_b, b_forget, ret_gn_g):
    x = np.asarray(x, np.float32)
    f = lambda a: np.ascontiguousarray(np.asarray(a, np.float32))
    w_in, w_out, w_ff1, w_ff2 = f(w_in), f(w_out), f(w_ff1), f(w_ff2)
    ln1_g, ln1_b, ln2_g, ln2_b, b_forget, ret_gn_g = f(ln1_g), f(ln1_b), f(ln2_g), f(ln2_b), f(b_forget), f(ret_gn_g)
    xT_b = [np.ascontiguousarray(x[b].T) for b in range(BATCH)]
    ncA = build_A()
    ncB = build_B()
    cores = list(range(NCORE))
    for l in range(DEPTH):
        ra = run_bass_kernel_spmd(ncA, a_inputs(xT_b, w_in[l], b_forget[l], ret_gn_g[l]), core_ids=cores)
        mixT = a_gather(ra.results)
        rb = run_bass_kernel_spmd(ncB, b_inputs(xT_b, mixT, w_out[l], w_ff1[l], w_ff2[l], ln1_g[l], ln1_b[l], ln2_g[l], ln2_b[l]), core_ids=cores)
        xT_b = b_gather(rb.results)
    out = np.stack([xT_b[b].T for b in range(BATCH)], axis=0)
    return np.ascontiguousarray(out.astype(np.float32))
```
